# Optimizing a Trainium2 kernel written in Bass

```python
import math
import jax, jax.numpy as jnp
from jax import lax
import numpy as np

D_MODEL = 1024
BATCH = 2
SEQ = 16384
DEPTH = 2

N_EVEN = (DEPTH + 1) // 2
N_ODD = DEPTH // 2

MEM_LEN = 256
NORM_EPS = 1e-6

MLA_HEADS = 8
MLA_NOPE = 64
MLA_ROPE = 32
MLA_QK = MLA_NOPE + MLA_ROPE
MLA_V = 64
MLA_Q_RANK = 384
MLA_KV_RANK = 256
ROPE_BASE = 10000.0
Q_BLOCK = 128

RW_HEADS = 8
RW_HEAD = 64
RW_DIM = RW_HEADS * RW_HEAD
RW_DECAY_LORA = 64
RW_AAA_LORA = 64
RW_GATE_LORA = 128
RW_LN_EPS = 64e-5

MLA_IN = MLA_Q_RANK + MLA_KV_RANK + MLA_ROPE
RW_IN = 3 * RW_DIM + RW_DECAY_LORA + RW_AAA_LORA + RW_GATE_LORA
EVEN_IN = MLA_IN + RW_IN
EVEN_MIX = MLA_HEADS * MLA_V + RW_DIM

SSM_INNER = 2 * D_MODEL
SSM_HEAD = 64
SSM_HEADS = SSM_INNER // SSM_HEAD
SSM_GROUPS = 4
SSM_STATE = 128
SSM_CONV = 4
SSM_CHUNK = 256
SSM_CONV_DIM = SSM_INNER + 2 * SSM_GROUPS * SSM_STATE
ODD_IN = SSM_INNER + SSM_CONV_DIM + SSM_HEADS

X_HEADS = 4
X_HEAD = 128
X_DIM = X_HEADS * X_HEAD

FFN_HIDDEN = -((-8 * D_MODEL) // (3 * 256)) * 256

kernel_name = "hybrid_mla_rwkv7_mamba2_memxattn"


def rms_norm(x, g, eps=NORM_EPS):
    xf = x.astype(jnp.float32)
    y = xf * lax.rsqrt(jnp.mean(xf * xf, axis=-1, keepdims=True) + eps)
    return (y * g.astype(jnp.float32)).astype(x.dtype)


def apply_rope(x, positions):
    half = x.shape[-1] // 2
    inv_freq = ROPE_BASE ** (-jnp.arange(half, dtype=jnp.float32) / half)
    ang = positions.astype(jnp.float32)[:, :, None, None] * inv_freq
    cos, sin = jnp.cos(ang), jnp.sin(ang)
    xf = x.astype(jnp.float32)
    x1, x2 = xf[..., :half], xf[..., half:]
    return jnp.concatenate([x1 * cos - x2 * sin, x2 * cos + x1 * sin], axis=-1).astype(x.dtype)


def causal_block_attention(q, k, v):
    b, s, h, dk = q.shape
    nb = s // Q_BLOCK
    scale = dk ** -0.5
    qb = jnp.moveaxis(q.reshape(b, nb, Q_BLOCK, h, dk), 1, 0)
    k_idx = jnp.arange(s)

    def block(args):
        q_blk, blk = args
        scores = jnp.einsum("bqhd,bkhd->bhqk", q_blk, k, preferred_element_type=jnp.float32) * scale
        q_idx = blk * Q_BLOCK + jnp.arange(Q_BLOCK)
        scores = jnp.where(k_idx[None, :] <= q_idx[:, None], scores, -jnp.inf)
        probs = jax.nn.softmax(scores, axis=-1).astype(v.dtype)
        return jnp.einsum("bhqk,bkhd->bqhd", probs, v)

    out = lax.map(block, (qb, jnp.arange(nb)))
    return jnp.moveaxis(out, 0, 1).reshape(b, s, h, v.shape[-1])


def mla_group(p, positions, q_norm, w_uq, kv_norm, w_ukv, q_hnorm, k_hnorm):
    b, s, _ = p.shape
    c_q = rms_norm(p[..., :MLA_Q_RANK], q_norm)
    c_kv = rms_norm(p[..., MLA_Q_RANK:MLA_Q_RANK + MLA_KV_RANK], kv_norm)
    k_rope = p[..., MLA_Q_RANK + MLA_KV_RANK:]
    q = (c_q @ w_uq).reshape(b, s, MLA_HEADS, MLA_QK)
    kv = (c_kv @ w_ukv).reshape(b, s, MLA_HEADS, MLA_NOPE + MLA_V)
    k_nope, v = kv[..., :MLA_NOPE], kv[..., MLA_NOPE:]
    k = jnp.concatenate([k_nope, jnp.broadcast_to(k_rope[:, :, None, :], (b, s, MLA_HEADS, MLA_ROPE))], axis=-1)
    q = rms_norm(q, q_hnorm)
    k = rms_norm(k, k_hnorm)
    q = jnp.concatenate([q[..., :MLA_NOPE], apply_rope(q[..., MLA_NOPE:], positions)], axis=-1)
    k = jnp.concatenate([k[..., :MLA_NOPE], apply_rope(k[..., MLA_NOPE:], positions)], axis=-1)
    return causal_block_attention(q, k, v).reshape(b, s, MLA_HEADS * MLA_V)


def token_shift(p, mu):
    prev = jnp.pad(p, ((0, 0), (1, 0), (0, 0)))[:, :-1]
    return p + (prev - p) * mu


def wkv7_scan(r, w, k, v, a, bb):
    b, s, h, n = r.shape
    xs = tuple(jnp.moveaxis(t.astype(jnp.float32), 1, 0) for t in (r, w, k, v, a, bb))

    def step(state, inp):
        r_t, w_t, k_t, v_t, a_t, b_t = inp
        sa = jnp.einsum("bhvk,bhk->bhv", state, a_t)
        state = state * w_t[:, :, None, :] + sa[..., None] * b_t[:, :, None, :] + v_t[..., None] * k_t[:, :, None, :]
        return state, jnp.einsum("bhvk,bhk->bhv", state, r_t)

    state0 = jnp.zeros((b, h, n, n), jnp.float32)
    _, ys = lax.scan(step, state0, xs)
    return jnp.moveaxis(ys, 0, 1)


def rwkv7_group(p, mu, w0, w2, a0, a2, g2, k_k, k_a, r_k, ln_g, ln_b):
    b, s, _ = p.shape
    p = token_shift(p, mu)
    r = p[..., :RW_DIM]
    k = p[..., RW_DIM:2 * RW_DIM]
    v = p[..., 2 * RW_DIM:3 * RW_DIM]
    o = 3 * RW_DIM
    xw = p[..., o:o + RW_DECAY_LORA]
    o += RW_DECAY_LORA
    xa = p[..., o:o + RW_AAA_LORA]
    o += RW_AAA_LORA
    xg = p[..., o:o + RW_GATE_LORA]
    w_log = -jax.nn.softplus(-(w0 + jnp.tanh(xw) @ w2)) - 0.5
    decay = jnp.exp(-jnp.exp(w_log.astype(jnp.float32)))
    a = jax.nn.sigmoid(a0 + xa @ a2)
    g = jax.nn.sigmoid(xg) @ g2
    heads = lambda t: t.reshape(b, s, RW_HEADS, RW_HEAD)
    kk = heads(k * k_k).astype(jnp.float32)
    kk = kk / jnp.maximum(jnp.sqrt(jnp.sum(kk * kk, axis=-1, keepdims=True)), 1e-12)
    k = k * (1.0 + (a - 1.0) * k_a)
    r_h, k_h, v_h, a_h = heads(r), heads(k), heads(v), heads(a).astype(jnp.float32)
    y = wkv7_scan(r_h, heads(decay), k_h, v_h, -kk, kk * a_h)
    mean = jnp.mean(y, axis=-1, keepdims=True)
    var = jnp.mean(jnp.square(y - mean), axis=-1, keepdims=True)
    y = ((y - mean) * lax.rsqrt(var + RW_LN_EPS)).reshape(b, s, RW_DIM)
    y = (y * ln_g.astype(jnp.float32) + ln_b.astype(jnp.float32)).astype(p.dtype)
    bonus = jnp.sum(r_h * k_h * r_k, axis=-1, keepdims=True) * v_h
    y = y + bonus.reshape(b, s, RW_DIM)
    return y * g


def even_mixer(h, positions, norm, w_in, q_norm, w_uq, kv_norm, w_ukv, q_hnorm, k_hnorm,
               mu, w0, w2, a0, a2, g2, k_k, k_a, r_k, ln_g, ln_b, w_out):
    proj = rms_norm(h, norm) @ w_in
    y_mla = mla_group(proj[..., :MLA_IN], positions, q_norm, w_uq, kv_norm, w_ukv, q_hnorm, k_hnorm)
    y_rw = rwkv7_group(proj[..., MLA_IN:], mu, w0, w2, a0, a2, g2, k_k, k_a, r_k, ln_g, ln_b)
    return jnp.concatenate([y_mla, y_rw], axis=-1) @ w_out


def causal_depthwise_conv(x, w, bias):
    width, c = w.shape
    out = lax.conv_general_dilated(x, w[:, None, :].astype(x.dtype), window_strides=(1,),
                                   padding=((width - 1, 0),),
                                   dimension_numbers=("NWC", "WIO", "NWC"),
                                   feature_group_count=c)
    return out + bias


def ssd_chunked(x, a, b_in, c_in):
    bsz, s, n_heads, head_p = x.shape
    n_groups, n_state = b_in.shape[2], b_in.shape[3]
    e_per = n_heads // n_groups
    L = SSM_CHUNK
    nc = -(-s // L)
    pad = nc * L - s
    padt = lambda t: jnp.pad(t.astype(jnp.float32), ((0, 0), (0, pad)) + ((0, 0),) * (t.ndim - 2))
    xc = padt(x).reshape(bsz, nc, L, n_groups, e_per, head_p)
    ac = padt(a).reshape(bsz, nc, L, n_groups, e_per)
    bc = padt(b_in).reshape(bsz, nc, L, n_groups, n_state)
    cc = padt(c_in).reshape(bsz, nc, L, n_groups, n_state)
    xs = tuple(jnp.moveaxis(t, 1, 0) for t in (xc, ac, bc, cc))
    causal = jnp.tril(jnp.ones((L, L), dtype=bool))[None, :, :, None, None]

    def step(state, inp):
        x_c, a_c, b_c, c_c = inp
        cum = jnp.cumsum(a_c, axis=1)
        seg = cum[:, :, None] - cum[:, None, :]
        decay = jnp.exp(jnp.where(causal, seg, -jnp.inf))
        cb = jnp.einsum("blgn,bsgn->blsg", c_c, b_c)
        y = jnp.einsum("blsg,blsge,bsgep->blgep", cb, decay, x_c)
        y = y + jnp.einsum("blgn,bgepn->blgep", c_c, state) * jnp.exp(cum)[..., None]
        to_end = jnp.exp(cum[:, -1:] - cum)
        state = state * jnp.exp(cum[:, -1])[..., None, None] + jnp.einsum("blgn,blge,blgep->bgepn", b_c, to_end, x_c)
        return state, y

    state0 = jnp.zeros((bsz, n_groups, e_per, head_p, n_state), jnp.float32)
    _, ys = lax.scan(step, state0, xs)
    return jnp.moveaxis(ys, 0, 1).reshape(bsz, nc * L, n_heads, head_p)[:, :s]


def mamba2_mixer(h, norm, w_in, conv_w, conv_b, dt_bias, a_log, d_skip, gnorm, w_out):
    b, s, _ = h.shape
    proj = rms_norm(h, norm) @ w_in
    z = proj[..., :SSM_INNER]
    xbc = proj[..., SSM_INNER:SSM_INNER + SSM_CONV_DIM]
    dt_raw = proj[..., SSM_INNER + SSM_CONV_DIM:]
    xbc = jax.nn.silu(causal_depthwise_conv(xbc, conv_w, conv_b))
    gn = SSM_GROUPS * SSM_STATE
    xs = xbc[..., :SSM_INNER].reshape(b, s, SSM_HEADS, SSM_HEAD)
    b_in = xbc[..., SSM_INNER:SSM_INNER + gn].reshape(b, s, SSM_GROUPS, SSM_STATE)
    c_in = xbc[..., SSM_INNER + gn:].reshape(b, s, SSM_GROUPS, SSM_STATE)
    dt = jax.nn.softplus(dt_raw.astype(jnp.float32) + dt_bias.astype(jnp.float32))
    A = -jnp.exp(a_log.astype(jnp.float32))
    xf = xs.astype(jnp.float32)
    y = ssd_chunked(xf * dt[..., None], dt * A, b_in, c_in)
    y = y + xf * d_skip.astype(jnp.float32)[:, None]
    y = y.reshape(b, s, SSM_INNER) * jax.nn.silu(z.astype(jnp.float32))
    y = y.reshape(b, s, SSM_GROUPS, SSM_INNER // SSM_GROUPS)
    y = y * lax.rsqrt(jnp.mean(y * y, axis=-1, keepdims=True) + NORM_EPS)
    y = (y.reshape(b, s, SSM_INNER) * gnorm.astype(jnp.float32)).astype(h.dtype)
    return y @ w_out


def memory_cross_attention(h, mem, norm_x, norm_mem, wq, wkv, q_hnorm, k_hnorm, wo):
    b, s, _ = h.shape
    m = mem.shape[1]
    q = (rms_norm(h, norm_x) @ wq).reshape(b, s, X_HEADS, X_HEAD)
    kv = (rms_norm(mem, norm_mem) @ wkv).reshape(b, m, 2, X_HEADS, X_HEAD)
    q = rms_norm(q, q_hnorm)
    k = rms_norm(kv[:, :, 0], k_hnorm)
    v = kv[:, :, 1]
    scores = jnp.einsum("bqhd,bmhd->bhqm", q, k, preferred_element_type=jnp.float32) * (X_HEAD ** -0.5)
    probs = jax.nn.softmax(scores, axis=-1).astype(v.dtype)
    out = jnp.einsum("bhqm,bmhd->bqhd", probs, v).reshape(b, s, X_DIM)
    return out @ wo


def swiglu_ffn(h, norm, w13, w2):
    u = rms_norm(h, norm) @ w13
    return (jax.nn.silu(u[..., :FFN_HIDDEN]) * u[..., FFN_HIDDEN:]) @ w2


def setup_inputs(seed: int = 0) -> dict:
    key = jax.random.key(seed)
    ks = iter(jax.random.split(key, 64))
    f32 = jnp.float32

    def nrm(shape, fan_in, scale=1.0):
        return scale * fan_in ** -0.5 * jax.random.normal(next(ks), shape, f32)

    def gain(shape):
        return 1.0 + 0.05 * jax.random.normal(next(ks), shape, f32)

    def unif(shape, lo, hi):
        return jax.random.uniform(next(ks), shape, f32, lo, hi)

    E, O, L = N_EVEN, N_ODD, DEPTH
    x = jax.random.normal(next(ks), (BATCH, SEQ, D_MODEL), f32)
    mem = jax.random.normal(next(ks), (BATCH, MEM_LEN, D_MODEL), f32)
    positions = (jax.random.randint(next(ks), (BATCH, 1), 0, 1024, jnp.int32)
                 + jnp.arange(SEQ, dtype=jnp.int32)[None, :])
    dt0 = jnp.exp(unif((O, SSM_HEADS), math.log(1e-3), math.log(1e-1)))
    return {
        "x": x, "mem": mem, "positions": positions,
        "ev_norm": gain((E, D_MODEL)),
        "ev_w_in": nrm((E, D_MODEL, EVEN_IN), D_MODEL),
        "mla_q_norm": gain((E, MLA_Q_RANK)),
        "mla_w_uq": nrm((E, MLA_Q_RANK, MLA_HEADS * MLA_QK), MLA_Q_RANK),
        "mla_kv_norm": gain((E, MLA_KV_RANK)),
        "mla_w_ukv": nrm((E, MLA_KV_RANK, MLA_HEADS * (MLA_NOPE + MLA_V)), MLA_KV_RANK),
        "mla_q_hnorm": gain((E, MLA_QK)),
        "mla_k_hnorm": gain((E, MLA_QK)),
        "rw_mu": unif((E, RW_IN), 0.0, 1.0),
        "rw_w0": unif((E, RW_DIM), -6.0, -1.0),
        "rw_w2": nrm((E, RW_DECAY_LORA, RW_DIM), RW_DECAY_LORA),
        "rw_a0": 0.1 * jax.random.normal(next(ks), (E, RW_DIM), f32),
        "rw_a2": nrm((E, RW_AAA_LORA, RW_DIM), RW_AAA_LORA),
        "rw_g2": nrm((E, RW_GATE_LORA, RW_DIM), RW_GATE_LORA),
        "rw_k_k": unif((E, RW_DIM), 0.7, 1.0),
        "rw_k_a": unif((E, RW_DIM), 0.8, 1.2),
        "rw_r_k": 0.1 * jax.random.normal(next(ks), (E, RW_HEADS, RW_HEAD), f32),
        "rw_ln_g": gain((E, RW_DIM)),
        "rw_ln_b": 0.02 * jax.random.normal(next(ks), (E, RW_DIM), f32),
        "ev_w_out": nrm((E, EVEN_MIX, D_MODEL), EVEN_MIX),
        "od_norm": gain((O, D_MODEL)),
        "od_w_in": nrm((O, D_MODEL, ODD_IN), D_MODEL),
        "ssm_conv_w": nrm((O, SSM_CONV, SSM_CONV_DIM), SSM_CONV),
        "ssm_conv_b": 0.02 * jax.random.normal(next(ks), (O, SSM_CONV_DIM), f32),
        "ssm_dt_bias": dt0 + jnp.log(-jnp.expm1(-dt0)),
        "ssm_a_log": jnp.log(unif((O, SSM_HEADS), 1.0, 16.0)),
        "ssm_d": gain((O, SSM_HEADS)),
        "ssm_gnorm": gain((O, SSM_INNER)),
        "od_w_out": nrm((O, SSM_INNER, D_MODEL), SSM_INNER),
        "xa_norm_x": gain((L, D_MODEL)),
        "xa_norm_mem": gain((L, D_MODEL)),
        "xa_wq": nrm((L, D_MODEL, X_DIM), D_MODEL),
        "xa_wkv": nrm((L, D_MODEL, 2 * X_DIM), D_MODEL),
        "xa_q_hnorm": gain((L, X_HEAD)),
        "xa_k_hnorm": gain((L, X_HEAD)),
        "xa_wo": nrm((L, X_DIM, D_MODEL), X_DIM),
        "ffn_norm": gain((L, D_MODEL)),
        "ffn_w13": nrm((L, D_MODEL, 2 * FFN_HIDDEN), D_MODEL),
        "ffn_w2": nrm((L, FFN_HIDDEN, D_MODEL), FFN_HIDDEN),
    }


def reference(x, mem, positions,
              ev_norm, ev_w_in, mla_q_norm, mla_w_uq, mla_kv_norm, mla_w_ukv, mla_q_hnorm, mla_k_hnorm,
              rw_mu, rw_w0, rw_w2, rw_a0, rw_a2, rw_g2, rw_k_k, rw_k_a, rw_r_k, rw_ln_g, rw_ln_b, ev_w_out,
              od_norm, od_w_in, ssm_conv_w, ssm_conv_b, ssm_dt_bias, ssm_a_log, ssm_d, ssm_gnorm, od_w_out,
              xa_norm_x, xa_norm_mem, xa_wq, xa_wkv, xa_q_hnorm, xa_k_hnorm, xa_wo,
              ffn_norm, ffn_w13, ffn_w2):
    h = x
    for i in range(DEPTH):
        j = i // 2
        if i % 2 == 0:
            h = h + even_mixer(h, positions, ev_norm[j], ev_w_in[j], mla_q_norm[j], mla_w_uq[j],
                               mla_kv_norm[j], mla_w_ukv[j], mla_q_hnorm[j], mla_k_hnorm[j],
                               rw_mu[j], rw_w0[j], rw_w2[j], rw_a0[j], rw_a2[j], rw_g2[j],
                               rw_k_k[j], rw_k_a[j], rw_r_k[j], rw_ln_g[j], rw_ln_b[j], ev_w_out[j])
        else:
            h = h + mamba2_mixer(h, od_norm[j], od_w_in[j], ssm_conv_w[j], ssm_conv_b[j],
                                 ssm_dt_bias[j], ssm_a_log[j], ssm_d[j], ssm_gnorm[j], od_w_out[j])
        h = h + memory_cross_attention(h, mem, xa_norm_x[i], xa_norm_mem[i], xa_wq[i], xa_wkv[i],
                                       xa_q_hnorm[i], xa_k_hnorm[i], xa_wo[i])
        h = h + swiglu_ffn(h, ffn_norm[i], ffn_w13[i], ffn_w2[i])
    return h
```

```python
import numpy as np
import concourse.bass as bass
import concourse.mybir as mybir
from concourse.bass_utils import run_bass_kernel_spmd

F32 = mybir.dt.float32
BF16 = mybir.dt.bfloat16
I32 = mybir.dt.int32
AF = mybir.ActivationFunctionType
ALU = mybir.AluOpType
AX = mybir.AxisListType


class R:
    __slots__ = ("name", "w", "rd", "excl")

    def __init__(self, name="", excl=False):
        self.name = name
        self.w = None
        self.rd = {}
        self.excl = excl


class V:
    __slots__ = ("ap", "r")

    def __init__(self, ap, r):
        self.ap = ap
        self.r = r

    def __getitem__(self, idx):
        return V(self.ap[idx], self.r)

    def m(self, f):
        return V(f(self.ap), self.r)

    def re(self, s, **kw):
        return V(self.ap.rearrange(s, **kw), self.r)


def _ap(x):
    return x.ap if isinstance(x, V) else x


class Prog:
    ENGS = ("pe", "act", "dve", "pool", "sp")
    LIMIT = 30000

    def __init__(self, nc, same_engine_sync=True):
        self.nc = nc
        self.q = {e: [] for e in self.ENGS}
        self.cnt = {}
        self.cur = {e: (e, 0) for e in self.ENGS}
        self.seen = {e: {} for e in self.ENGS}
        self.chan = {}
        self.ses = same_engine_sync
        self.n_alloc = 0
        self.clear_sems = True

    def sb(self, name, shape, dtype=F32):
        h = self.nc.alloc_sbuf_tensor(name, list(shape), dtype)
        return V(h[tuple(slice(None) for _ in shape)], R(name))

    def ps(self, name, shape, dtype=F32):
        h = self.nc.alloc_psum_tensor(name, list(shape), dtype)
        return V(h[tuple(slice(None) for _ in shape)], R(name, excl=True))

    def dram(self, name, shape, dtype, kind="Internal"):
        h = self.nc.dram_tensor(name, list(shape), dtype, kind=kind)
        return V(h.ap(), R(name))

    def _waits(self, eng, reads, writes):
        need = {}

        def add(dep):
            if dep is None:
                return
            k, v = dep
            if need.get(k, 0) < v:
                need[k] = v

        for r in reads:
            add(r.w)
            if r.excl:
                for k, v in r.rd.items():
                    if k[0] != eng:
                        add((k, v))
        for w in writes:
            add(w.w)
            for k, v in w.rd.items():
                add((k, v))
        out = []
        for k, v in need.items():
            if k[0] == eng and (eng == "pe" or not self.ses):
                continue
            if self.seen[eng].get(k, 0) >= v:
                continue
            self.seen[eng][k] = v
            out.append((k, v))
        return out

    def _bump(self, key_holder, name, inc):
        k = key_holder[name]
        c = self.cnt.get(k, 0)
        if c + inc > self.LIMIT:
            k = (k[0], k[1] + 1)
            key_holder[name] = k
            c = 0
        c += inc
        self.cnt[k] = c
        return k, c

    def _res(self, xs):
        out = []
        for x in xs:
            if x is None or isinstance(x, (int, float)):
                continue
            r = x.r if isinstance(x, V) else x
            if r is not None and r not in out:
                out.append(r)
        return out

    def op(self, eng, emit, reads=(), writes=()):
        reads = self._res(reads)
        writes = self._res(writes)
        waits = self._waits(eng, reads, writes)
        k, c = self._bump(self.cur, eng, 1)
        self.q[eng].append((waits, emit, k, 1))
        for r in reads:
            if r.rd.get(k, 0) < c:
                r.rd[k] = c
        for w in writes:
            w.w = (k, c)
            w.rd = {}
        return (k, c)

    NDMA = 24

    def dma(self, eng, out, in_, chan="d", **kw):
        reads = self._res([in_])
        writes = self._res([out])
        slot = "dq%d" % (self.n_alloc % self.NDMA)
        self.n_alloc += 1
        if slot not in self.chan:
            self.chan[slot] = (slot, 0)
        prev = self.chan[slot]
        pc_ = self.cnt.get(prev, 0)
        waits = self._waits(eng, reads, writes)
        if pc_ > 0 and self.seen[eng].get(prev, 0) < pc_:
            self.seen[eng][prev] = pc_
            waits.append((prev, pc_))
        k, c = self._bump(self.chan, slot, 16)
        o, i = _ap(out), _ap(in_)
        self.q[eng].append((waits, lambda e: e.dma_start(out=o, in_=i, **kw), k, 16))
        for r in reads:
            if r.rd.get(k, 0) < c:
                r.rd[k] = c
        for w in writes:
            w.w = (k, c)
            w.rd = {}
        return (k, c)

    def mm(self, out, lhsT, rhs, start=True, stop=True, **kw):
        o, l, r = _ap(out), _ap(lhsT), _ap(rhs)
        return self.op("pe", lambda e: e.matmul(o, l, r, start=start, stop=stop, **kw),
                       reads=[lhsT, rhs] + ([] if start else [out]), writes=[out])

    def transpose(self, out, in_, ident):
        o, i, d = _ap(out), _ap(in_), _ap(ident)
        return self.op("pe", lambda e: e.transpose(o, i, d), reads=[in_, ident], writes=[out])

    def act(self, out, in_, func, bias=None, scale=None, accum_out=None, eng="act"):
        o, i = _ap(out), _ap(in_)
        kw = {}
        if bias is not None:
            kw["bias"] = _ap(bias)
        if scale is not None:
            kw["scale"] = _ap(scale)
        if accum_out is not None:
            kw["accum_out"] = _ap(accum_out)
        return self.op("act", lambda e: e.activation(o, i, func, **kw),
                       reads=[in_, bias, scale], writes=[out, accum_out])

    def tt(self, eng, out, in0, in1, op):
        o, a, b = _ap(out), _ap(in0), _ap(in1)
        return self.op(eng, lambda e: e.tensor_tensor(o, a, b, op), reads=[in0, in1], writes=[out])

    def ts(self, eng, out, in0, s1, s2, op0, op1=None, accum_out=None):
        o, a = _ap(out), _ap(in0)
        x1, x2 = _ap(s1), _ap(s2)
        kw = {}
        if op1 is not None:
            kw["op1"] = op1
        if accum_out is not None:
            kw["accum_out"] = _ap(accum_out)
        return self.op(eng, lambda e: e.tensor_scalar(o, a, x1, x2, op0, **kw),
                       reads=[in0, s1, s2], writes=[out, accum_out])

    def stt(self, eng, out, in0, scalar, in1, op0, op1):
        o, a, s, b = _ap(out), _ap(in0), _ap(scalar), _ap(in1)
        eng = "dve"
        return self.op(eng, lambda e: e.scalar_tensor_tensor(o, a, s, b, op0, op1),
                       reads=[in0, scalar, in1], writes=[out])

    def copy(self, eng, out, in_):
        o, i = _ap(out), _ap(in_)
        if eng == "act":
            return self.op(eng, lambda e: e.copy(o, i), reads=[in_], writes=[out])
        return self.op(eng, lambda e: e.tensor_copy(o, i), reads=[in_], writes=[out])

    def memset(self, eng, out, val):
        o = _ap(out)
        return self.op(eng, lambda e: e.memset(o, val), writes=[out])

    def reduce(self, eng, out, in_, op, axis=None):
        o, i = _ap(out), _ap(in_)
        ax = axis if axis is not None else AX.X
        return self.op(eng, lambda e: e.tensor_reduce(o, i, ax, op), reads=[in_], writes=[out])

    def emit(self):
        nc = self.nc
        final = []
        for k, c in self.cnt.items():
            final.append((k, c))
        keys = list(self.cnt.keys())
        sems = {k: nc.alloc_semaphore("s_%s_%d" % (k[0].replace(":", "_"), k[1])) for k in keys}
        self.sems = sems
        names = {"pe": "tensor", "act": "scalar", "dve": "vector", "pool": "gpsimd", "sp": "sync"}
        q = self.q
        if self.clear_sems:
            for k in keys:
                nc.sync.sem_clear(sems[k])
            nc.all_engine_barrier()
        with nc.Block() as block:
            def mk(eng):
                def body(e):
                    for waits, emit, k, inc in q[eng]:
                        for wk, wv in waits:
                            e.wait_ge(sems[wk], wv)
                        ins = emit(e)
                        ins.then_inc(sems[k], inc)
                    if eng == "sp":
                        for fk, fc in final:
                            e.wait_ge(sems[fk], fc)
                return body
            for eng in self.ENGS:
                if q[eng] or eng == "sp":
                    getattr(block, names[eng])(mk(eng))
        return nc

    def stats(self):
        return {e: len(self.q[e]) for e in self.ENGS}

import math
import ml_dtypes

NPBF = ml_dtypes.bfloat16
TW = 512
MAGIC = 12582912.0
TWO_PI = float(2 * np.pi)


class Bld:
    def __init__(self):
        self.nc = bass.Bass("TRN2", target_bir_lowering=False)
        self.P = Prog(self.nc)
        self.pools = {}
        self.nps = 0
        self.psl = []

    def inp(self, name, shape, dtype=F32):
        return V(self.nc.dram_tensor(name, list(shape), dtype, kind="ExternalInput").ap(), None)

    def out(self, name, shape, dtype=F32):
        return self.P.dram(name, shape, dtype, kind="ExternalOutput")

    def pool(self, name, shape, dtype, count):
        tiles = [self.P.sb("%s%d" % (name, i), shape, dtype) for i in range(count)]
        self.pools[name] = [tiles, 0]

    def get(self, name):
        pl = self.pools[name]
        t = pl[0][pl[1] % len(pl[0])]
        pl[1] += 1
        return t

    def psum(self):
        if not self.psl:
            self.psl = [self.P.ps("psb%d" % i, [128, 512], F32) for i in range(8)]
        t = self.psl[self.nps % 8]
        self.nps += 1
        return t

    def const(self, name, arr_shape, dtype, eng="sp"):
        d = self.inp(name, arr_shape, dtype)
        s = self.P.sb("c_" + name, arr_shape, dtype)
        self.P.dma(eng, s, d, chan="c")
        return s

    def wres(self, name, K, N):
        d = self.inp(name, [K, N], F32)
        kp = min(K, 128)
        kc = K // kp
        s = self.P.sb("w_" + name, [kp, kc, N], BF16)
        for c in range(kc):
            self.P.dma("pool", s[:, c, :], d[c * kp:(c + 1) * kp, :], chan="w")
        return s

    def cval(self, name, val, parts=128):
        t = self.P.sb("cv_" + name, [parts, 1], F32)
        self.P.memset("pool", t, val)
        return t


def pc(v, p=128):
    v = np.asarray(v, dtype=np.float32).reshape(-1, p)
    return np.ascontiguousarray(v.T)


def rstd_from_ps(b, ps_ap, n, parts, scale, eps_t, name="rstd"):
    P = b.P
    r = b.get(name)
    P.act(r[:parts, :n], ps_ap, AF.Sqrt, bias=eps_t[:parts, 0:1], scale=scale)
    o, i = r.ap[:parts, :n], r.ap[:parts, :n]
    P.op("dve", lambda e: e.reciprocal(o, i), reads=[r], writes=[r])
    return r


def build_front0(ntok=4096, TW=256, dbg=()):
    b = Bld()
    P = b.P
    NT = ntok + 1
    xT = b.inp("xT", [1024, NT])
    pos = b.inp("pos", [1, NT], I32)
    Win = b.wres("ev_w_in", 1024, 2464)
    Wuq = b.wres("mla_w_uq", 384, 768)
    Wukv = b.wres("mla_w_ukv", 256, 1024)
    W2 = b.wres("rw_w2", 64, 512)
    A2 = b.wres("rw_a2", 64, 512)
    G2 = b.wres("rw_g2", 128, 512)
    g_ev = b.const("g_ev", [128, 8], F32)
    g_q = b.const("g_q", [128, 3], F32)
    g_kv = b.const("g_kv", [128, 2], F32)
    g_hq = b.const("g_hq", [96, 1], F32)
    g_hk = b.const("g_hk", [96, 1], F32)
    mu = b.const("mu", [128, 15], F32)
    w0 = b.const("w0", [128, 4], F32)
    a0 = b.const("a0", [128, 4], F32)
    k_k = b.const("k_k", [128, 4], F32)
    k_a = b.const("k_a", [128, 4], F32)
    r_k = b.const("r_k", [128, 4], F32)
    invf = b.const("invf", [96, 1], F32)
    permT = b.const("permT", [96, 96], BF16)
    ones = b.const("ones", [128, 128], BF16)
    blk64 = b.const("blk64", [128, 128], BF16)
    eps6 = b.cval("eps6", 1e-6)
    zero = b.cval("zero", 0.0)

    qT_o = b.out("qT", [8, 96, ntok], BF16)
    kT_o = b.out("kT", [8, 96, ntok], BF16)
    vT_o = b.out("vT", [512, ntok], BF16)
    rw_o = b.out("rw", [8, 512, ntok], F32)

    b.pool("xt", [128, 8, TW], F32, 2)
    b.pool("sq", [128, 8, TW], BF16, 2)
    b.pool("xn", [128, 8, TW], BF16, 1)
    b.pool("rstd", [128, TW], F32, 3)
    b.pool("cq", [128, 3, TW], F32, 2)
    b.pool("cqn", [128, 3, TW], BF16, 1)
    b.pool("ckvn", [128, 2, TW], BF16, 1)
    b.pool("hsq", [96, TW], BF16, 2)
    b.pool("hn", [96, TW], BF16, 2)
    b.pool("ht1", [96, TW], F32, 2)
    b.pool("ht2", [96, TW], F32, 2)
    b.pool("ho", [96, TW], BF16, 3)
    b.pool("kh", [96, TW], F32, 2)
    b.pool("kr", [32, TW], F32, 2)
    b.pool("vst", [128, 4, TW], BF16, 2)
    b.pool("ang", [96, TW], F32, 4)
    b.pool("posi", [96, TW], I32, 2)
    b.pool("cs", [96, 2, TW], F32, 2)
    PR = P.sb("PR", [128, 15, TW + 1], F32)
    P.memset("pool", PR, 0.0)
    b.pool("SH", [128, 15, TW], F32, 1)
    b.pool("dd", [128, TW], F32, 3)
    b.pool("bfa", [128, TW], BF16, 4)
    b.pool("lora", [128, TW], BF16, 3)
    b.pool("f32a", [128, TW], F32, 6)
    b.pool("OUT", [128, 8, TW], F32, 2)

    slots = [(672 + 128 * j, 128) for j in range(12)] + [(2208, 64), (2272, 64), (2336, 128)]
    engs2 = ["dve", "pool"]

    def head_norm_rope(src, gain, cs, n, dst):
        stage = 99
        for f in dbg:
            if f.startswith('st'):
                stage = int(f[2:])
        hsq = b.get("hsq")
        P.act(hsq[:, :n], src, AF.Square)
        if stage < 2:
            return
        pst = b.psum()
        P.mm(pst[:96, :n], ones[:96, :96], hsq[:, :n])
        if stage < 3:
            return
        rs = rstd_from_ps(b, pst[:96, :n], n, 96, 1.0 / 96, eps6)
        if stage < 4:
            return
        hn = b.get("hn")
        P.stt("dve", hn[:, :n], src, gain[:, 0:1], rs[:96, :n], ALU.mult, ALU.mult)
        if stage < 5:
            return
        prot = b.psum()
        P.mm(prot[:96, :n], permT, hn[:, :n])
        t1 = b.get("ht1")
        P.tt("pool", t1[:, :n], hn[:, :n], cs[:, 0, :n], ALU.mult)
        if stage < 6:
            return
        t2 = b.get("ht2")
        P.tt("dve", t2[:, :n], prot[:96, :n], cs[:, 1, :n], ALU.mult)
        ho = b.get("ho")
        P.tt("pool", ho[:, :n], t1[:, :n], t2[:, :n], ALU.add)
        if stage < 7:
            return
        if dst is not None:
            P.dma("sp", dst, ho[:, :n], chan="o")

    def sincos(n, t0):
        if 'nosc' in dbg:
            cs = b.get("cs")
            P.memset("pool", cs, 0.5)
            return cs
        pi_ = b.get("posi")
        P.dma("sp", pi_[:, :n], pos[0:1, t0:t0 + n].m(lambda a: a.partition_broadcast(96)), chan="x")
        ang = b.get("ang")
        P.copy("dve", ang[:, :n], pi_[:, :n])
        P.ts("dve", ang[:, :n], ang[:, :n], invf[:, 0:1], None, ALU.mult)
        cs = b.get("cs")
        for which, shift in ((1, 0.0), (0, float(np.pi / 2))):
            a2 = b.get("ang")
            if shift:
                P.ts("pool", a2[:, :n], ang[:, :n], shift, None, ALU.add)
            else:
                P.copy("pool", a2[:, :n], ang[:, :n])
            nn = b.get("ang")
            P.ts("dve", nn[:, :n], a2[:, :n], 1.0 / TWO_PI, MAGIC, ALU.mult, ALU.add)
            P.ts("dve", nn[:, :n], nn[:, :n], MAGIC, None, ALU.subtract)
            P.stt("dve", a2[:, :n], nn[:, :n], -TWO_PI, a2[:, :n], ALU.mult, ALU.add)
            P.ts("dve", a2[:, :n], a2[:, :n], 3.1415925, -3.1415925, ALU.min, ALU.max)
            P.act(cs[:, which, :n], a2[:, :n], AF.Sin)
        return cs

    def tile(t0, n, write, tok0):
        xt = b.get("xt")
        P.dma("sp", xt[:, :, :n], xT.re("(c p) t -> p c t", p=128)[:, :, t0:t0 + n], chan="x",
              **({"allow_slow_non_contiguous": True} if n < 16 else {}))
        sq = b.get("sq")
        for c in range(8):
            P.act(sq[:, c, :n], xt[:, c, :n], AF.Square)
        pss = b.psum()
        for c in range(8):
            P.mm(pss[:, :n], ones, sq[:, c, :n], start=(c == 0), stop=(c == 7))
        rstd = rstd_from_ps(b, pss[:, :n], n, 128, 1.0 / 1024, eps6)
        xn = b.get("xn")
        for c in range(8):
            P.stt(engs2[c % 2], xn[:, c, :n], xt[:, c, :n], g_ev[:, c:c + 1], rstd[:, :n], ALU.mult, ALU.mult)

        def lin(ps, W, c0, m, rhs, KC):
            for kc in range(KC):
                P.mm(ps[:m, :n], W[:, kc, c0:c0 + m], rhs[:, kc, :n], start=(kc == 0), stop=(kc == KC - 1))

        if write and 'nomla' not in dbg:
            cs = sincos(n, t0)
            cq = b.get("cq")
            sqq = b.get("sq")
            for j in range(3):
                p = b.psum()
                lin(p, Win, 128 * j, 128, xn, 8)
                P.copy("dve", cq[:, j, :n], p[:, :n])
                P.act(sqq[:, j, :n], p[:, :n], AF.Square)
            pss = b.psum()
            for j in range(3):
                P.mm(pss[:, :n], ones, sqq[:, j, :n], start=(j == 0), stop=(j == 2))
            rq = rstd_from_ps(b, pss[:, :n], n, 128, 1.0 / 384, eps6)
            cqn = b.get("cqn")
            for j in range(3):
                P.stt(engs2[j % 2], cqn[:, j, :n], cq[:, j, :n], g_q[:, j:j + 1], rq[:, :n], ALU.mult, ALU.mult)
            for h in range(0 if 'noq' in dbg else 8):
                p = b.psum()
                lin(p, Wuq, 96 * h, 96, cqn, 3)
                head_norm_rope(p[:96, :n], g_hq, cs, n, qT_o[h, :, tok0:tok0 + n])
            ckv = b.get("cq")
            sqq = b.get("sq")
            for j in range(2):
                p = b.psum()
                lin(p, Win, 384 + 128 * j, 128, xn, 8)
                P.copy("dve", ckv[:, j, :n], p[:, :n])
                P.act(sqq[:, j, :n], p[:, :n], AF.Square)
            pss = b.psum()
            for j in range(2):
                P.mm(pss[:, :n], ones, sqq[:, j, :n], start=(j == 0), stop=(j == 1))
            rkv = rstd_from_ps(b, pss[:, :n], n, 128, 1.0 / 256, eps6)
            ckvn = b.get("ckvn")
            for j in range(2):
                P.stt(engs2[j % 2], ckvn[:, j, :n], ckv[:, j, :n], g_kv[:, j:j + 1], rkv[:, :n], ALU.mult, ALU.mult)
            p = b.psum()
            lin(p, Win, 640, 32, xn, 8)
            kr = b.get("kr")
            P.copy("act", kr[:, :n], p[:32, :n])
            vst = b.get("vst")
            for h in range(0 if 'nokv' in dbg else 8):
                p = b.psum()
                lin(p, Wukv, 128 * h, 128, ckvn, 2)
                kh = b.get("kh")
                P.copy("act", kh[0:64, :n], p[0:64, :n])
                P.copy("dve", kh[64:96, :n], kr[0:32, :n])
                r0 = (h % 2) * 64
                P.copy("dve", vst[r0:r0 + 64, h // 2, :n], p[64:128, :n])
                head_norm_rope(kh[:, :n], g_hk, cs, n, kT_o[h, :, tok0:tok0 + n])
            P.dma("sp", vT_o.re("(c p) t -> p c t", p=128)[:, :, tok0:tok0 + n], vst[:, :, :n], chan="o")

        for s, (c0, m) in enumerate(slots):
            p = b.psum()
            lin(p, Win, c0, m, xn, 8)
            if s % 2 == 0:
                P.copy("act", PR[:m, s, 1:1 + n], p[:m, :n])
            else:
                P.copy("dve", PR[:m, s, 1:1 + n], p[:m, :n])
        if write and 'norw' not in dbg:
            SH = b.get("SH")
            for s, (c0, m) in enumerate(slots):
                d = b.get("dd")
                e1 = engs2[s % 2]
                P.tt(e1, d[:m, :n], PR[:m, s, 0:n], PR[:m, s, 1:1 + n], ALU.subtract)
                P.stt(e1, SH[:m, s, :n], d[:m, :n], mu[:m, s:s + 1], PR[:m, s, 1:1 + n], ALU.mult, ALU.add)
            th = b.get("lora")
            P.act(th[:64, :n], SH[:64, 12, :n], AF.Tanh)
            xab = b.get("lora")
            P.copy("dve", xab[:64, :n], SH[:64, 13, :n])
            sg = b.get("lora")
            P.act(sg[:, :n], SH[:, 14, :n], AF.Sigmoid)
            for j in range(4):
                OUT = b.get("OUT")
                r_sh = SH[:, j, :n]
                k_sh = SH[:, 4 + j, :n]
                v_sh = SH[:, 8 + j, :n]
                p = b.psum()
                P.mm(p[:, :n], W2[:, 0, 128 * j:128 * j + 128], th[:64, :n])
                ee = b.get("f32a")
                P.act(ee[:, :n], p[:, :n], AF.Sigmoid, bias=w0[:, j:j + 1])
                P.ts("pool", OUT[:, 5, :n], ee[:, :n], float(math.exp(-0.5)), None, ALU.mult)
                p = b.psum()
                P.mm(p[:, :n], A2[:, 0, 128 * j:128 * j + 128], xab[:64, :n])
                aa = b.get("f32a")
                P.act(aa[:, :n], p[:, :n], AF.Sigmoid, bias=a0[:, j:j + 1])
                p = b.psum()
                P.mm(p[:, :n], G2[:, 0, 128 * j:128 * j + 128], sg[:, :n])
                P.copy("act", OUT[:, 6, :n], p[:, :n])
                kkr = b.get("f32a")
                P.ts("pool", kkr[:, :n], k_sh, k_k[:, j:j + 1], None, ALU.mult)
                sqk = b.get("bfa")
                P.act(sqk[:, :n], kkr[:, :n], AF.Square)
                p = b.psum()
                P.mm(p[:, :n], blk64, sqk[:, :n])
                nrm = b.get("f32a")
                P.act(nrm[:, :n], p[:, :n], AF.Sqrt, bias=zero[:, 0:1], scale=1.0)
                P.ts("dve", nrm[:, :n], nrm[:, :n], 1e-12, None, ALU.max)
                o_, i_ = nrm.ap[:, :n], nrm.ap[:, :n]
                P.op("dve", lambda e, o_=o_, i_=i_: e.reciprocal(o_, i_), reads=[nrm], writes=[nrm])
                kk = b.get("f32a")
                P.tt("pool", kk[:, :n], kkr[:, :n], nrm[:, :n], ALU.mult)
                P.ts("pool", OUT[:, 3, :n], kk[:, :n], -1.0, None, ALU.mult)
                P.tt("pool", OUT[:, 4, :n], kk[:, :n], aa[:, :n], ALU.mult)
                tmp = b.get("f32a")
                P.ts("dve", tmp[:, :n], aa[:, :n], -1.0, k_a[:, j:j + 1], ALU.add, ALU.mult)
                P.stt("dve", OUT[:, 1, :n], tmp[:, :n], 1.0, k_sh, ALU.add, ALU.mult)
                P.copy("pool", OUT[:, 0, :n], r_sh)
                P.copy("pool", OUT[:, 2, :n], v_sh)
                rk = b.get("bfa")
                P.stt("dve", rk[:, :n], r_sh, r_k[:, j:j + 1], OUT[:, 1, :n], ALU.mult, ALU.mult)
                p = b.psum()
                P.mm(p[:, :n], blk64, rk[:, :n])
                P.tt("dve", OUT[:, 7, :n], p[:, :n], v_sh, ALU.mult)
                P.dma("sp", rw_o.re("q (c p) t -> p q c t", p=128)[:, :, j, tok0:tok0 + n],
                      OUT[:, :, :n], chan="o")
        P.copy("dve", PR[:, :, 0:1], PR[:, :, n:n + 1])

    tile(0, 1, False, 0)
    for i in range(ntok // TW):
        tile(1 + i * TW, TW, True, i * TW)
    P.emit()
    return b


def front0_consts():
    half = 16
    invf16 = (10000.0 ** (-np.arange(half, dtype=np.float32) / half)).astype(np.float32)
    invf = np.zeros((96, 1), np.float32)
    invf[64:80, 0] = invf16
    invf[80:96, 0] = invf16
    permT = np.zeros((96, 96), np.float32)
    for i in range(16):
        permT[80 + i, 64 + i] = -1.0
        permT[64 + i, 80 + i] = 1.0
    blk = np.zeros((128, 128), np.float32)
    blk[:64, :64] = 1
    blk[64:, 64:] = 1
    return {"invf": invf, "permT": permT.astype(NPBF), "ones": np.ones((128, 128), NPBF), "blk64": blk.astype(NPBF)}


def front0_inputs(inp, core):
    bi, seg = core // 4, core % 4
    ntok = 4096
    x = inp["x"][bi]
    s0 = seg * ntok
    xs = np.zeros((ntok + 1, 1024), np.float32)
    if s0 > 0:
        xs[0] = x[s0 - 1]
    xs[1:] = x[s0:s0 + ntok]
    ps = np.zeros((1, ntok + 1), np.int32)
    ps[0, 1:] = inp["positions"][bi, s0:s0 + ntok]
    mu = inp["rw_mu"][0]
    mu15 = np.zeros((128, 15), np.float32)
    mu15[:, :12] = pc(mu[:1536])
    mu15[:64, 12] = mu[1536:1600]
    mu15[:64, 13] = mu[1600:1664]
    mu15[:, 14] = mu[1664:1792]
    d = {
        "xT": np.ascontiguousarray(xs.T), "pos": ps,
        "ev_w_in": inp["ev_w_in"][0], "mla_w_uq": inp["mla_w_uq"][0], "mla_w_ukv": inp["mla_w_ukv"][0],
        "rw_w2": inp["rw_w2"][0], "rw_a2": inp["rw_a2"][0], "rw_g2": inp["rw_g2"][0],
        "g_ev": pc(inp["ev_norm"][0]), "g_q": pc(inp["mla_q_norm"][0]), "g_kv": pc(inp["mla_kv_norm"][0]),
        "g_hq": pc(inp["mla_q_hnorm"][0], 96), "g_hk": pc(inp["mla_k_hnorm"][0], 96),
        "mu": mu15, "w0": pc(inp["rw_w0"][0]), "a0": pc(inp["rw_a0"][0]), "k_k": pc(inp["rw_k_k"][0]),
        "k_a": pc(inp["rw_k_a"][0]), "r_k": pc(inp["rw_r_k"][0].reshape(-1)),
    }
    d.update(front0_consts())
    return d


def build_attn(S=16384):
    b = Bld()
    P = b.P
    NB = S // 128
    qT = b.inp("qT", [2, 96, S], BF16)
    kT = b.inp("kT", [2, 96, S], BF16)
    vp = b.inp("vp", [2, 128, NB, 128], BF16)
    masks = b.const("masks", [128, 4, 512], BF16)
    yT = b.out("yT", [2, 64, S], BF16)
    K = P.sb("K", [96, S], BF16)
    Vt = P.sb("Vt", [128, NB, 128], BF16)
    b.pool("q", [96, 512], BF16, 2)
    b.pool("pt", [128, 512], BF16, 4)
    b.pool("rc", [64, 512], F32, 2)
    b.pool("o", [64, 512], BF16, 2)
    pso = [P.ps("pso%d" % i, [128, 512], F32) for i in range(2)]
    pss = [P.ps("pss%d" % i, [128, 512], F32) for i in range(6)]
    scale = 96 ** -0.5
    cnt = 0
    for h in range(2):
        for c in range(4):
            P.dma("sp", K[:, c * (S // 4):(c + 1) * (S // 4)], kT[h, :, c * (S // 4):(c + 1) * (S // 4)], chan="k")
            P.dma("sp", Vt[:, c * (NB // 4):(c + 1) * (NB // 4), :], vp[h, :, c * (NB // 4):(c + 1) * (NB // 4), :], chan="k")
        for qi in range(S // 512):
            q = b.get("q")
            P.dma("sp", q, qT[h, :, qi * 512:(qi + 1) * 512], chan="q")
            po = pso[qi % 2]
            nkb = 4 * qi + 4
            LA = 2
            pend = []

            def issue_s(kb):
                nonlocal cnt
                ps = pss[cnt % 6]
                cnt += 1
                P.mm(ps, K[:, kb * 128:(kb + 1) * 128], q)
                pt = b.get("pt")
                P.act(pt, ps, AF.Exp, scale=scale)
                if kb >= 4 * qi:
                    P.tt("pool" if kb % 2 else "dve", pt, pt, masks[:, kb - 4 * qi, :], ALU.mult)
                pend.append((kb, pt))
            for kb in range(min(LA, nkb)):
                issue_s(kb)
            for kb in range(nkb):
                if kb + LA < nkb:
                    issue_s(kb + LA)
                kb0, pt = pend.pop(0)
                P.mm(po, Vt[:, kb0, :], pt, start=(kb0 == 0), stop=(kb0 == nkb - 1))
            rc = b.get("rc")
            o_, i_ = rc.ap, po.ap[64:128, :]
            P.op("dve", lambda e, o_=o_, i_=i_: e.reciprocal(o_, i_), reads=[po], writes=[rc])
            o = b.get("o")
            P.tt("dve", o, po[0:64, :], rc, ALU.mult)
            P.dma("pool", yT[h, :, qi * 512:(qi + 1) * 512], o, chan="o")
    P.emit()
    return b


def attn_masks():
    m = np.zeros((128, 4, 512), np.float32)
    k = np.arange(128)[:, None]
    q = np.arange(512)[None, :]
    for j in range(4):
        m[:, j, :] = (128 * j + k <= q)
    return m.astype(NPBF)


def build_back(kind, ntok=4096, TW=256, dbg=()):
    b = Bld()
    P = b.P
    NTL = ntok // TW
    hT_in = b.inp("hT", [1024, ntok])
    memT = b.inp("memT", [1024, 256])
    hT_out = b.out("hT_out", [1024, ntok])
    ones = b.const("ones", [128, 128], BF16)
    blk64f = b.const("blk64f", [128, 128], F32)
    eps6 = b.cval("eps6", 1e-6)
    if kind == "even":
        ymT = b.inp("ymT", [512, ntok], BF16)
        ysT = b.inp("ysT", [512, ntok])
        gT = b.inp("gT", [512, ntok])
        bonT = b.inp("bonT", [512, ntok])
        ln_g = b.const("ln_g", [128, 4], F32)
        ln_b = b.const("ln_b", [128, 4], F32)
        epsln = b.cval("epsln", 64e-5)
        Wout = b.wres("w_out", 1024, 1024)
        KO = 8
    else:
        yT_in = b.inp("yT", [2048, ntok])
        zsT = b.inp("zsT", [2048, ntok], BF16)
        gn = b.const("gnorm", [128, 16], F32)
        Wout = b.wres("w_out", 2048, 1024)
        KO = 16
    Wq = b.wres("xa_wq", 1024, 512)
    Wkv = b.wres("xa_wkv", 1024, 1024)
    Wo = b.wres("xa_wo", 512, 1024)
    g_x = b.const("g_x", [128, 8], F32)
    g_m = b.const("g_m", [128, 8], F32)
    g_hq = b.const("g_hq", [128, 1], F32)
    g_hk = b.const("g_hk", [128, 1], F32)
    g_f = b.const("g_f", [128, 8], F32)
    w13 = b.inp("ffn_w13", [1024, 5632])
    w2 = b.inp("ffn_w2", [2816, 1024])
    w13s = P.dram("w13s", [22, 128, 2, 8, 128], BF16)
    w2s = P.dram("w2s", [8, 128, 22, 128], BF16)
    for j in range(22):
        for g in range(2):
            P.dma("pool", w13s[j, :, g, :, :],
                  w13.re("(c p) n -> p c n", p=128)[:, :, g * 2816 + j * 128: g * 2816 + (j + 1) * 128], chan="wc")
    for oc in range(8):
        P.dma("pool", w2s[oc], w2.re("(c p) n -> p c n", p=128)[:, :, oc * 128:(oc + 1) * 128], chan="wc")

    b.pool("h", [128, 8, TW], F32, 2)
    b.pool("sq", [128, 8, TW], BF16, 1)
    b.pool("xn", [128, 8, TW], BF16, 1)
    b.pool("rstd", [128, TW], F32, 2)
    b.pool("mix", [128, KO, TW], BF16, 1)
    b.pool("f32", [128, TW], F32, 6)
    b.pool("bf", [128, TW], BF16, 4)
    b.pool("qn", [128, 4, TW], BF16, 1)
    b.pool("pt", [128, 2, TW], BF16, 2)
    b.pool("oa", [128, 4, TW], BF16, 1)
    b.pool("act", [128, 22, TW], BF16, 1)
    b.pool("wg", [128, 2, 8, 128], BF16, 3)
    b.pool("w2b", [128, 22, 128], BF16, 2)
    engs2 = ["dve", "pool"]

    def rms_to_bf(src, gains, n, nchunks, dst, scale):
        sq = b.get("sq")
        for c in range(nchunks):
            P.act(sq[:, c, :n], src[:, c, :n], AF.Square)
        pss = b.psum()
        for c in range(nchunks):
            P.mm(pss[:, :n], ones, sq[:, c, :n], start=(c == 0), stop=(c == nchunks - 1))
        rstd = rstd_from_ps(b, pss[:, :n], n, 128, scale, eps6)
        for c in range(nchunks):
            P.stt("dve", dst[:, c, :n], src[:, c, :n], gains[:, c:c + 1], rstd[:, :n], ALU.mult, ALU.mult)

    mt = P.sb("memt", [128, 8, 256], F32)
    P.dma("sp", mt, memT.re("(c p) t -> p c t", p=128), chan="x")
    memn = P.sb("memn", [128, 8, 256], BF16)
    rms_to_bf(mt, g_m, 256, 8, memn, 1.0 / 1024)
    KhT = P.sb("KhT", [128, 4, 256], BF16)
    Vx = P.sb("Vx", [128, 2, 512], BF16)
    for h in range(4):
        p = b.psum()
        for kc in range(8):
            P.mm(p[:, :256], Wkv[:, kc, 128 * h:128 * h + 128], memn[:, kc, :], start=(kc == 0), stop=(kc == 7))
        kf = b.get("f32")
        P.copy("dve", kf[:, :256], p[:, :256])
        sqk = b.get("bf")
        P.act(sqk[:, :256], kf[:, :256], AF.Square)
        p2 = b.psum()
        P.mm(p2[:, :256], ones, sqk[:, :256])
        rs = rstd_from_ps(b, p2[:, :256], 256, 128, 1.0 / 128, eps6)
        P.stt("dve", KhT[:, h, :], kf[:, :256], g_hk[:, 0:1], rs[:, :256], ALU.mult, ALU.mult)
    for m in range(2):
        p = b.psum()
        for kc in range(8):
            P.mm(p[:, :512], memn[:, kc, m * 128:(m + 1) * 128], Wkv[:, kc, 512:1024], start=(kc == 0), stop=(kc == 7))
        P.copy("dve", Vx[:, m, :], p[:, :512])

    xscale = 128 ** -0.5
    for ti in range(NTL):
        n = TW
        t0 = ti * TW
        h = b.get("h")
        P.dma("sp", h, hT_in.re("(c p) t -> p c t", p=128)[:, :, t0:t0 + n], chan="x")
        mix = b.get("mix")
        if kind == "even":
            P.dma("act", mix[:, 0:4, :], ymT.re("(c p) t -> p c t", p=128)[:, :, t0:t0 + n], chan="x2")
            for j in range(4):
                ys = b.get("f32")
                P.dma("sp", ys, ysT[128 * j:128 * (j + 1), t0:t0 + n], chan="x")
                gg = b.get("f32")
                P.dma("sp", gg, gT[128 * j:128 * (j + 1), t0:t0 + n], chan="x")
                bo = b.get("f32")
                P.dma("sp", bo, bonT[128 * j:128 * (j + 1), t0:t0 + n], chan="x")
                p = b.psum()
                P.mm(p[:, :n], blk64f, ys[:, :n])
                cen = b.get("f32")
                P.stt("dve", cen[:, :n], p[:, :n], -1.0 / 64, ys[:, :n], ALU.mult, ALU.add)
                sq = b.get("f32")
                P.tt("pool", sq[:, :n], cen[:, :n], cen[:, :n], ALU.mult)
                p2 = b.psum()
                P.mm(p2[:, :n], blk64f, sq[:, :n])
                rs = rstd_from_ps(b, p2[:, :n], n, 128, 1.0 / 64, epsln)
                P.tt("pool", cen[:, :n], cen[:, :n], rs[:, :n], ALU.mult)
                P.ts("dve", cen[:, :n], cen[:, :n], ln_g[:, j:j + 1], ln_b[:, j:j + 1], ALU.mult, ALU.add)
                P.tt("pool", cen[:, :n], cen[:, :n], bo[:, :n], ALU.add)
                P.tt("pool", mix[:, 4 + j, :n], cen[:, :n], gg[:, :n], ALU.mult)
        else:
            zs = b.get("act")
            P.dma("act", zs[:, 0:16, :], zsT.re("(c p) t -> p c t", p=128)[:, :, t0:t0 + n], chan="x2")
            for grp in range(4):
                ygs = []
                p2 = b.psum()
                for jj in range(4):
                    j = grp * 4 + jj
                    y = b.get("f32")
                    P.dma("sp", y, yT_in[128 * j:128 * (j + 1), t0:t0 + n], chan="x")
                    P.tt("pool", y[:, :n], y[:, :n], zs[:, j, :n], ALU.mult)
                    sq = b.get("bf")
                    P.act(sq[:, :n], y[:, :n], AF.Square)
                    P.mm(p2[:, :n], ones, sq[:, :n], start=(jj == 0), stop=(jj == 3))
                    ygs.append(y)
                rs = rstd_from_ps(b, p2[:, :n], n, 128, 1.0 / 512, eps6)
                for jj in range(4):
                    j = grp * 4 + jj
                    P.stt("dve", mix[:, j, :n], ygs[jj][:, :n], gn[:, j:j + 1], rs[:, :n], ALU.mult, ALU.mult)
        for oc in range(8):
            p = b.psum()
            for kc in range(KO):
                P.mm(p[:, :n], Wout[:, kc, 128 * oc:128 * oc + 128], mix[:, kc, :n], start=(kc == 0), stop=(kc == KO - 1))
            P.tt("dve", h[:, oc, :n], h[:, oc, :n], p[:, :n], ALU.add)
        if 'mixonly' in dbg:
            P.dma("pool", hT_out.re("(c p) t -> p c t", p=128)[:, :, t0:t0 + n], h, chan="o")
            continue
        xn = b.get("xn")
        rms_to_bf(h, g_x, n, 8, xn, 1.0 / 1024)
        qn = b.get("qn")
        for hh in range(4):
            p = b.psum()
            for kc in range(8):
                P.mm(p[:, :n], Wq[:, kc, 128 * hh:128 * hh + 128], xn[:, kc, :n], start=(kc == 0), stop=(kc == 7))
            qf = b.get("f32")
            P.copy("dve", qf[:, :n], p[:, :n])
            sqq = b.get("bf")
            P.act(sqq[:, :n], qf[:, :n], AF.Square)
            p2 = b.psum()
            P.mm(p2[:, :n], ones, sqq[:, :n])
            rs = rstd_from_ps(b, p2[:, :n], n, 128, 1.0 / 128, eps6)
            P.stt("dve", qn[:, hh, :n], qf[:, :n], g_hq[:, 0:1], rs[:, :n], ALU.mult, ALU.mult)
        oa = b.get("oa")
        for hh in range(4):
            pt = b.get("pt")
            for m in range(2):
                p = b.psum()
                P.mm(p[:, :n], KhT[:, hh, m * 128:(m + 1) * 128], qn[:, hh, :n])
                P.act(pt[:, m, :n], p[:, :n], AF.Exp, scale=xscale)
            po = b.psum()
            pd = b.psum()
            for m in range(2):
                P.mm(po[:, :n], Vx[:, m, 128 * hh:128 * hh + 128], pt[:, m, :n], start=(m == 0), stop=(m == 1))
            for m in range(2):
                P.mm(pd[:, :n], ones, pt[:, m, :n], start=(m == 0), stop=(m == 1))
            rc = b.get("f32")
            o_, i_ = rc.ap[:, :n], pd.ap[:, :n]
            P.op("dve", lambda e, o_=o_, i_=i_: e.reciprocal(o_, i_), reads=[pd], writes=[rc])
            P.tt("dve", oa[:, hh, :n], po[:, :n], rc[:, :n], ALU.mult)
        for oc in range(8):
            p = b.psum()
            for kc in range(4):
                P.mm(p[:, :n], Wo[:, kc, 128 * oc:128 * oc + 128], oa[:, kc, :n], start=(kc == 0), stop=(kc == 3))
            P.tt("dve", h[:, oc, :n], h[:, oc, :n], p[:, :n], ALU.add)
        xn = b.get("xn")
        rms_to_bf(h, g_f, n, 8, xn, 1.0 / 1024)
        act = b.get("act")
        for j in range(22):
            wg = b.get("wg")
            P.dma("sp" if j % 2 == 0 else "act", wg, w13s[j], chan="ws%d" % (j % 2))
            pg = b.psum()
            pu = b.psum()
            for kc in range(8):
                P.mm(pg[:, :n], wg[:, 0, kc, :], xn[:, kc, :n], start=(kc == 0), stop=(kc == 7))
            for kc in range(8):
                P.mm(pu[:, :n], wg[:, 1, kc, :], xn[:, kc, :n], start=(kc == 0), stop=(kc == 7))
            sg = b.get("f32")
            P.act(sg[:, :n], pg[:, :n], AF.Silu)
            P.tt("dve", act[:, j, :n], sg[:, :n], pu[:, :n], ALU.mult)
        for oc in range(8):
            w2b = b.get("w2b")
            P.dma("sp" if oc % 2 == 0 else "act", w2b, w2s[oc], chan="ws%d" % (oc % 2))
            p = b.psum()
            for kc in range(22):
                P.mm(p[:, :n], w2b[:, kc, :], act[:, kc, :n], start=(kc == 0), stop=(kc == 21))
            P.tt("dve", h[:, oc, :n], h[:, oc, :n], p[:, :n], ALU.add)
        P.dma("pool", hT_out.re("(c p) t -> p c t", p=128)[:, :, t0:t0 + n], h, chan="o")
    P.emit()
    return b


def build_rwkv(S=16384, G=4, SC=8):
    b = Bld()
    P = b.P
    C = 64
    NCH = S // C
    NSC = NCH // SC
    fm = b.inp("fm", [NSC, 128, 4, SC * C])
    etm = b.inp("etm", [NSC, 64, SC, 128])
    st = b.inp("st", [NSC, 128, 5, SC, 64])
    y_o = b.out("y", [NSC, 128, SC, 64])
    triI = b.const("triI", [64, 64], F32)
    triS = b.const("triS", [64, 64], F32)
    triS_bd = b.const("triS_bd", [128, 128], F32)
    triR_bd = b.const("triR_bd", [128, 128], F32)
    maskT = b.const("maskT", [128, 256], F32)
    maskN = b.const("maskN", [128, 128], F32)
    ident = b.const("ident", [128, 128], F32)

    def slot_tiles(nm, shape, zero=False):
        ts_ = [P.sb("%s_%d" % (nm, g), shape, F32) for g in range(G)]
        if zero:
            for t in ts_:
                P.memset("pool", t, 0.0)
        return ts_

    EX = slot_tiles("EX", [128, 3, 64])
    EXs = slot_tiles("EXs", [128, 2, 64])
    AR = slot_tiles("AR", [128, 256], True)
    Bb = slot_tiles("Bb", [128, 128], True)
    Kb = slot_tiles("Kb", [128, 128], True)
    X0 = slot_tiles("X0", [128, 192], True)
    X1 = slot_tiles("X1", [128, 192])
    X2 = slot_tiles("X2", [128, 192])
    Bh = slot_tiles("Bh", [128, 128], True)
    Kh = slot_tiles("Kh", [128, 128], True)
    AT1 = slot_tiles("AT1", [128, 256])
    AT2 = slot_tiles("AT2", [128, 256])
    Nn = [slot_tiles("Nn%d" % i, [128, 128]) for i in range(2)]
    NTn = [slot_tiles("NTn%d" % i, [128, 128]) for i in range(2)]
    Msb = slot_tiles("Msb", [128, 128])
    QT = slot_tiles("QT", [128, 128])
    STs = [P.sb("ST%d" % i, [128, 64], F32) for i in range(2)]
    P.memset("pool", STs[0], 0.0)
    b.pool("fmb", [128, 4, SC * C], F32, 2)
    b.pool("etb", [64, SC, 128], F32, 2)
    b.pool("stb", [128, 5, SC, 64], F32, 2)
    b.pool("yb", [128, SC, 64], F32, 2)
    bufs = {}
    state = {"cur": 0}

    def load_sc(sc):
        fmb = b.get("fmb")
        etb = b.get("etb")
        stb = b.get("stb")
        yb = b.get("yb")
        P.dma("sp", fmb, fm[sc], chan="in")
        P.dma("sp", etb, etm[sc], chan="in")
        P.dma("sp", stb, st[sc], chan="in")
        bufs[sc] = (fmb, etb, stb, yb)

    def chunk(c):
        g = c % G
        sc, ci = divmod(c, SC)
        fmb, etb, stb, yb = bufs[sc]
        cs = slice(ci * C, (ci + 1) * C)
        ex, exs = EX[g], EXs[g]
        p1 = b.psum()
        P.mm(p1[:, 0:64], etb[:, ci, :], triI)
        P.mm(p1[:, 64:128], etb[:, ci, :], triS)
        P.act(ex[:, 0, :], p1[:, 0:64], AF.Exp)
        P.act(ex[:, 1, :], p1[:, 0:64], AF.Exp, scale=-1.0)
        P.act(ex[:, 2, :], p1[:, 64:128], AF.Exp, scale=-1.0)
        yield
        p2 = b.psum()
        P.mm(p2[:, 0:64], triS_bd, stb[:, 0, ci, :])
        P.mm(p2[:, 64:128], triR_bd, stb[:, 0, ci, :])
        P.act(exs[:, 0, :], p2[:, 0:64], AF.Exp, scale=-1.0)
        P.act(exs[:, 1, :], p2[:, 64:128], AF.Exp, scale=-1.0)
        yield
        ar, bb, kb, x0, bh, kh = AR[g], Bb[g], Kb[g], X0[g], Bh[g], Kh[g]
        for hd in range(2):
            ps_ = slice(hd * 64, hd * 64 + 64)
            co = hd * 64
            P.tt("pool", ar[ps_, co:co + 64], fmb[ps_, 2, cs], ex[ps_, 2, :], ALU.mult)
            P.tt("pool", ar[ps_, 128 + co:128 + co + 64], fmb[ps_, 0, cs], ex[ps_, 1, :], ALU.mult)
            P.tt("pool", bb[ps_, co:co + 64], fmb[ps_, 3, cs], ex[ps_, 0, :], ALU.mult)
            P.tt("pool", kb[ps_, co:co + 64], fmb[ps_, 1, cs], ex[ps_, 0, :], ALU.mult)
            P.tt("dve", x0[ps_, co:co + 64], stb[ps_, 1, ci, :], exs[ps_, 0, :], ALU.mult)
            P.tt("dve", bh[ps_, co:co + 64], stb[ps_, 2, ci, :], exs[ps_, 1, :], ALU.mult)
            P.tt("pool", kh[ps_, co:co + 64], stb[ps_, 3, ci, :], exs[ps_, 1, :], ALU.mult)
        yield
        at1, at2 = AT1[g], AT2[g]
        pa = b.psum()
        P.mm(pa[:, 0:256], bb, ar)
        P.tt("dve", at1, pa[:, 0:256], maskT, ALU.mult)
        pa2 = b.psum()
        P.mm(pa2[:, 0:256], kb, ar)
        P.tt("dve", at2, pa2[:, 0:256], maskT, ALU.mult)
        pa3 = b.psum()
        P.mm(pa3[:, 0:128], ar[:, 0:128], bb)
        ncur, ntcur = Nn[0][g], at1[:, 0:128]
        P.tt("dve", ncur, pa3[:, 0:128], maskN, ALU.mult)
        yield
        p5 = b.psum()
        P.mm(p5[:, 0:64], at2[:, 0:128], stb[:, 4, ci, :])
        P.copy("act", x0[:, 128:192], p5[:, 0:64])
        yield
        xcur = x0
        xs = [X1[g], X2[g]]
        for j in range(6):
            px = b.psum()
            P.mm(px[:, 0:192], ntcur, xcur)
            xn = xs[j % 2]
            P.tt("dve", xn, px[:, 0:192], xcur, ALU.add)
            xcur = xn
            if j < 5:
                pn = b.psum()
                P.mm(pn[:, 0:128], ncur, ntcur)
                P.mm(pn[:, 128:256], ntcur, ncur)
                nt2 = NTn[j % 2][g]
                n2 = Nn[(j + 1) % 2][g]
                P.copy("act", nt2, pn[:, 0:128])
                P.copy("act", n2, pn[:, 128:256])
                ncur, ntcur = n2, nt2
            yield
        pm = b.psum()
        P.mm(pm[:, 0:128], xcur[:, 0:128], bh)
        P.stt("dve", Msb[g], ident, ex[:, 1, 63:64], pm[:, 0:128], ALU.mult, ALU.add)
        pq = b.psum()
        P.mm(pq[:, 0:128], xcur[:, 0:128], at1[:, 128:256])
        P.tt("dve", QT[g], pq[:, 0:128], ar[:, 128:256], ALU.add)
        yield
        s_old = STs[state["cur"]]
        s_new = STs[1 - state["cur"]]
        state["cur"] = 1 - state["cur"]
        py = b.psum()
        P.mm(py[:, 0:64], at1[:, 128:256], xcur[:, 128:192], start=True, stop=False)
        P.mm(py[:, 0:64], at2[:, 128:256], stb[:, 4, ci, :], start=False, stop=False)
        P.mm(py[:, 0:64], QT[g], s_old, start=False, stop=True)
        P.copy("act", yb[:, ci, :], py[:, 0:64])
        pst = b.psum()
        P.mm(pst[:, 0:64], bh, xcur[:, 128:192], start=True, stop=False)
        P.mm(pst[:, 0:64], kh, stb[:, 4, ci, :], start=False, stop=False)
        P.mm(pst[:, 0:64], Msb[g], s_old, start=False, stop=True)
        P.copy("dve", s_new, pst[:, 0:64])
        if ci == SC - 1:
            P.dma("pool", y_o[sc], yb, chan="o")
        yield

    load_sc(0)
    for c0 in range(0, NCH, G):
        sc_next = (c0 // SC) + 1
        if c0 % SC == 0 and sc_next < NSC:
            load_sc(sc_next)
        gens = [chunk(c) for c in range(c0, c0 + G)]
        live = list(gens)
        while live:
            nxt = []
            for gnr in live:
                try:
                    next(gnr)
                    nxt.append(gnr)
                except StopIteration:
                    pass
            live = nxt
    P.emit()
    return b


def rwkv_consts():
    C = 64
    i = np.arange(C)
    triI = (i[:, None] <= i[None, :]).astype(np.float32)
    triS = (i[:, None] < i[None, :]).astype(np.float32)
    triR = (i[:, None] > i[None, :]).astype(np.float32)

    def bd(m):
        o = np.zeros((128, 128), np.float32)
        o[:64, :64] = m
        o[64:, 64:] = m
        return o
    return {"triI": triI, "triS": triS, "triS_bd": bd(triS), "triR_bd": bd(triR),
            "maskT": np.concatenate([bd(triS), bd(triI)], 1), "maskN": bd(triR),
            "ident": np.eye(128, dtype=np.float32)}


def rwkv_inputs(rw_b, hp, S, SC=8):
    C = 64
    NCH = S // C
    ch = slice(hp * 128, hp * 128 + 128)
    NSC = NCH // SC
    fm = np.stack([rw_b[0, ch], rw_b[1, ch], rw_b[3, ch], rw_b[4, ch]])
    fm = np.ascontiguousarray(fm.reshape(4, 128, NSC, SC * C).transpose(2, 1, 0, 3))
    e = rw_b[5, ch]
    etm = e.T.reshape(NSC, SC, C, 128).transpose(0, 2, 1, 3)
    etm = np.ascontiguousarray(etm)

    def stack(x):
        x = x.reshape(2, 64, NCH, C)
        return x.transpose(2, 0, 3, 1).reshape(NCH, 128, 64)
    st = np.stack([stack(rw_b[5, ch]), stack(rw_b[3, ch]), stack(rw_b[4, ch]),
                   stack(rw_b[1, ch]), stack(rw_b[2, ch])])
    st = np.ascontiguousarray(st.reshape(5, NSC, SC, 128, 64).transpose(1, 3, 0, 2, 4))
    d = {"fm": fm, "etm": etm, "st": st}
    d.update(rwkv_consts())
    return d


def rwkv_unstack(y, S):
    NCH = S // 64
    NSC = y.shape[0]
    y = y.transpose(0, 2, 1, 3).reshape(NCH, 128, 64)
    return np.ascontiguousarray(y.reshape(NCH, 2, 64, 64).transpose(1, 3, 0, 2).reshape(128, S))


def build_front1(ntok=4096, TW=256):
    b = Bld()
    P = b.P
    NT = ntok + 3
    hT = b.inp("hT", [1024, NT])
    Win = b.wres("od_w_in", 1024, 5152)
    g_od = b.const("g_od", [128, 8], F32)
    cw = b.const("conv_w", [128, 24, 4], F32)
    cb = b.const("conv_b", [128, 24], F32)
    dtb = b.const("dt_bias", [32, 1], F32)
    ones = b.const("ones", [128, 128], BF16)
    eps6 = b.cval("eps6", 1e-6)
    one = b.cval("one", 1.0)
    zs_o = b.out("zsT", [2048, ntok], BF16)
    xbc_o = b.out("xbcT", [3072, ntok], F32)
    dt_o = b.out("dtT", [32, ntok], F32)
    b.pool("xt", [128, 8, TW], F32, 2)
    b.pool("sq", [128, 8, TW], BF16, 1)
    b.pool("xn", [128, 8, TW], BF16, 1)
    b.pool("rstd", [128, TW], F32, 2)
    XB = P.sb("XB", [128, 24, TW + 3], F32)
    P.memset("pool", XB, 0.0)
    b.pool("zs", [128, 16, TW], BF16, 1)
    b.pool("acc", [128, TW], F32, 3)
    b.pool("xo", [128, 24, TW], F32, 1)
    b.pool("dt", [32, TW], F32, 2)

    def tile(t0, n, write, tok0):
        xt = b.get("xt")
        P.dma("sp", xt[:, :, :n], hT.re("(c p) t -> p c t", p=128)[:, :, t0:t0 + n], chan="x",
              **({"allow_slow_non_contiguous": True} if n < 16 else {}))
        sq = b.get("sq")
        for c in range(8):
            P.act(sq[:, c, :n], xt[:, c, :n], AF.Square)
        pss = b.psum()
        for c in range(8):
            P.mm(pss[:, :n], ones, sq[:, c, :n], start=(c == 0), stop=(c == 7))
        rstd = rstd_from_ps(b, pss[:, :n], n, 128, 1.0 / 1024, eps6)
        xn = b.get("xn")
        for c in range(8):
            P.stt("dve", xn[:, c, :n], xt[:, c, :n], g_od[:, c:c + 1], rstd[:, :n], ALU.mult, ALU.mult)

        def lin(ps, c0, m):
            for kc in range(8):
                P.mm(ps[:m, :n], Win[:, kc, c0:c0 + m], xn[:, kc, :n], start=(kc == 0), stop=(kc == 7))
        for j in range(24):
            p = b.psum()
            lin(p, 2048 + 128 * j, 128)
            if j % 2 == 0:
                P.copy("act", XB[:, j, 3:3 + n], p[:, :n])
            else:
                P.copy("dve", XB[:, j, 3:3 + n], p[:, :n])
        if write:
            zs = b.get("zs")
            for j in range(16):
                p = b.psum()
                lin(p, 128 * j, 128)
                P.act(zs[:, j, :n], p[:, :n], AF.Silu)
            P.dma("sp", zs_o.re("(c p) t -> p c t", p=128)[:, :, tok0:tok0 + n], zs[:, :, :n], chan="o")
            xo = b.get("xo")
            for j in range(24):
                acc = b.get("acc")
                P.ts("pool", acc[:, :n], XB[:, j, 0:n], cw[:, j, 0:1], None, ALU.mult)
                for k in range(1, 4):
                    P.stt("dve", acc[:, :n], XB[:, j, k:k + n], cw[:, j, k:k + 1], acc[:, :n], ALU.mult, ALU.add)
                P.act(xo[:, j, :n], acc[:, :n], AF.Silu, bias=cb[:, j:j + 1])
            P.dma("sp", xbc_o.re("(c p) t -> p c t", p=128)[:, :, tok0:tok0 + n], xo[:, :, :n], chan="o")
            p = b.psum()
            lin(p, 5120, 32)
            dt = b.get("dt")
            P.act(dt[:, :n], p[:32, :n], AF.Exp, bias=dtb[:, 0:1])
            P.act(dt[:, :n], dt[:, :n], AF.Ln, bias=one[:32, 0:1])
            P.dma("sp", dt_o[:, tok0:tok0 + n], dt[:, :n], chan="o")
        P.copy("dve", XB[:, :, 0:3], XB[:, :, n:n + 3])

    tile(0, 3, False, 0)
    for i in range(ntok // TW):
        tile(3 + i * TW, TW, True, i * TW)
    P.emit()
    return b


def build_ssd(S=16384, SC=8):
    b = Bld()
    P = b.P
    L = 128
    NC = S // L
    NSC = NC // SC
    x_tm = b.inp("x_tm", [NSC, 128, SC, 512])
    B_tm = b.inp("B_tm", [NSC, 128, SC, 128])
    dt_tm = b.inp("dt_tm", [NSC, 128, SC, 8])
    BT = b.inp("BT", [NSC, 128, SC * L])
    CT = b.inp("CT", [NSC, 128, SC * L])
    y_o = b.out("y", [NSC, 128, SC, 512])
    alog = b.const("alog", [128, 8], F32)
    Dsk = b.const("Dsk", [128, 8], F32)
    triI = b.const("triI", [128, 128], F32)
    ntriI = b.const("ntriI", [128, 128], F32)
    triR = b.const("triR", [128, 128], F32)
    onesf = b.const("onesf", [128, 128], F32)
    Aneg = P.sb("Aneg", [128, 8], F32)
    P.act(Aneg, alog, AF.Exp)
    P.ts("dve", Aneg, Aneg, -1.0, None, ALU.mult)
    STs = [P.sb("ST%d" % i, [128, 512], F32) for i in range(2)]
    P.memset("pool", STs[0], 0.0)
    G = 3
    b.pool("xb", [128, SC, 512], F32, 3)
    b.pool("bb", [128, SC, 128], F32, 3)
    b.pool("dtb", [128, SC, 8], F32, 3)
    b.pool("btb", [128, SC * L], F32, 3)
    b.pool("ctb", [128, SC * L], F32, 3)
    b.pool("yb", [128, SC, 512], F32, 3)

    def slot(nm, shape, cnt=1):
        return [[P.sb("%s_%d_%d" % (nm, g, i), shape, F32) for i in range(cnt)] for g in range(G)]
    A_ = slot("a", [128, 8])
    SM = slot("sm", [128, 8], 3)
    GM = slot("gm", [128, 128])
    AT = slot("atri", [128, 2, 512], 2)
    DC = slot("dcl", [128, 512], 2)
    MT = slot("mt", [128, 128], 4)
    YI = slot("yi", [128, 512])
    XH = slot("xh", [128, 512])
    pyis = [P.ps("pyi%d" % g, [128, 512], F32) for g in range(G)]
    b.psl = [P.ps("psr%d" % i, [128, 512], F32) for i in range(5)]

    def psum5():
        t = b.psl[b.nps % 5]
        b.nps += 1
        return t
    state = {"cur": 0}
    bufs = {}

    def load_sc(sc):
        xb, bb, dtb_, btb, ctb, yb = (b.get(k) for k in ("xb", "bb", "dtb", "btb", "ctb", "yb"))
        P.dma("sp", xb, x_tm[sc], chan="in")
        P.dma("sp", bb, B_tm[sc], chan="in")
        P.dma("sp", dtb_, dt_tm[sc], chan="in")
        P.dma("act", btb, BT[sc], chan="in2")
        P.dma("act", ctb, CT[sc], chan="in2")
        bufs[sc] = (xb, bb, dtb_, btb, ctb, yb)

    def chunk(c):
        g = c % G
        sc, ci = divmod(c, SC)
        xb, bb, dtb_, btb, ctb, yb = bufs[sc]
        cs = slice(ci * L, (ci + 1) * L)
        a = A_[g][0]
        P.tt("pool", a, dtb_[:, ci, :], Aneg, ALU.mult)
        psm = psum5()
        P.mm(psm[:, 0:8], triI, a)
        P.mm(psm[:, 8:16], triR, a)
        P.mm(psm[:, 16:24], onesf, a)
        ecum, wgt, eL = SM[g]
        P.act(ecum, psm[:, 0:8], AF.Exp)
        P.act(wgt, psm[:, 8:16], AF.Exp)
        P.act(eL, psm[:, 16:24], AF.Exp)
        P.tt("pool", wgt, wgt, dtb_[:, ci, :], ALU.mult)
        yield
        pg = psum5()
        P.mm(pg[:, 0:128], btb[:, cs], ctb[:, cs])
        gm = GM[g][0]
        P.tt("dve", gm, pg[:, 0:128], triI, ALU.mult)
        yield
        pyi = pyis[g]
        for half in range(2):
            at = AT[g][half]
            for hh in range(4):
                h = half * 4 + hh
                P.ts("pool", at[:, 0, hh * 128:(hh + 1) * 128], triI, a[:, h:h + 1], None, ALU.mult)
                P.ts("pool", at[:, 1, hh * 128:(hh + 1) * 128], onesf, a[:, h:h + 1], None, ALU.mult)
            pd = psum5()
            P.mm(pd, onesf, at[:, 0, :], start=True, stop=False)
            P.mm(pd, ntriI, at[:, 1, :], start=False, stop=True)
            dcl = DC[g][half]
            P.ts("dve", dcl, pd, 0.0, None, ALU.min)
            P.act(dcl, dcl, AF.Exp)
            yield
            for hh in range(4):
                h = half * 4 + hh
                mt = MT[g][hh]
                P.stt("dve", mt, dcl[:, hh * 128:(hh + 1) * 128], dtb_[:, ci, h:h + 1], gm, ALU.mult, ALU.mult)
                P.mm(pyi[:, h * 64:(h + 1) * 64], mt, xb[:, ci, h * 64:(h + 1) * 64])
            yield
        yi = YI[g][0]
        P.copy("act", yi, pyi)
        xh = XH[g][0]
        for h in range(8):
            hs = slice(h * 64, (h + 1) * 64)
            P.ts("pool", xh[:, hs], xb[:, ci, hs], wgt[:, h:h + 1], None, ALU.mult)
        yield
        s_old, s_new = STs[state["cur"]], STs[1 - state["cur"]]
        state["cur"] = 1 - state["cur"]
        pyo = psum5()
        P.mm(pyo, ctb[:, cs], s_old)
        pst = psum5()
        P.mm(pst, bb[:, ci, :], xh)
        for h in range(8):
            hs = slice(h * 64, (h + 1) * 64)
            P.stt("dve", s_new[:, hs], s_old[:, hs], eL[:, h:h + 1], pst[:, hs], ALU.mult, ALU.add)
        for h in range(8):
            hs = slice(h * 64, (h + 1) * 64)
            P.stt("dve", yi[:, hs], pyo[:, hs], ecum[:, h:h + 1], yi[:, hs], ALU.mult, ALU.add)
            P.stt("dve", yb[:, ci, hs], xb[:, ci, hs], Dsk[:, h:h + 1], yi[:, hs], ALU.mult, ALU.add)
        if ci == SC - 1:
            P.dma("pool", y_o[sc], yb, chan="o")
        yield

    load_sc(0)
    live = []
    nxt_c = 0
    loaded = 0
    while nxt_c < NC or live:
        while len(live) < G and nxt_c < NC:
            sc_need = nxt_c // SC
            if sc_need + 1 < NSC and loaded < sc_need + 1 and nxt_c % SC == 0:
                load_sc(sc_need + 1)
                loaded = sc_need + 1
            live.append(chunk(nxt_c))
            nxt_c += 1
        keep = []
        for gnr in live:
            try:
                next(gnr)
                keep.append(gnr)
            except StopIteration:
                pass
        live = keep
    P.emit()
    return b


def ssd_consts():
    i = np.arange(128)
    triI = (i[:, None] <= i[None, :]).astype(np.float32)
    triR = (i[:, None] > i[None, :]).astype(np.float32)
    return {"triI": triI, "ntriI": -triI, "triR": triR, "onesf": np.ones((128, 128), np.float32)}


def ssd_inputs(xbcT_b, dtT_b, a_log, d_skip, g, S, SC=8):
    L = 128
    NC = S // L
    NSC = NC // SC
    x = xbcT_b[512 * g:512 * g + 512]
    x_tm = x.T.reshape(NSC, SC, L, 512).transpose(0, 2, 1, 3)
    Bm = xbcT_b[2048 + 128 * g:2048 + 128 * g + 128]
    Cm = xbcT_b[2560 + 128 * g:2560 + 128 * g + 128]
    B_tm = Bm.T.reshape(NSC, SC, L, 128).transpose(0, 2, 1, 3)
    dt = dtT_b[8 * g:8 * g + 8]
    dt_tm = dt.T.reshape(NSC, SC, L, 8).transpose(0, 2, 1, 3)
    BTs = Bm.reshape(128, NSC, SC * L).transpose(1, 0, 2)
    CTs = Cm.reshape(128, NSC, SC * L).transpose(1, 0, 2)
    d = {"x_tm": np.ascontiguousarray(x_tm), "B_tm": np.ascontiguousarray(B_tm), "dt_tm": np.ascontiguousarray(dt_tm),
         "BT": np.ascontiguousarray(BTs), "CT": np.ascontiguousarray(CTs),
         "alog": np.ascontiguousarray(np.broadcast_to(a_log[8 * g:8 * g + 8][None, :], (128, 8))).astype(np.float32),
         "Dsk": np.ascontiguousarray(np.broadcast_to(d_skip[8 * g:8 * g + 8][None, :], (128, 8))).astype(np.float32)}
    d.update(ssd_consts())
    return d


def ssd_unstack(y, S):
    NSC, _, SC, _ = y.shape
    return y.transpose(0, 2, 1, 3).reshape(S, 512)


def back_common(inp, i, bi):
    blk = np.zeros((128, 128), np.float32)
    blk[:64, :64] = 1
    blk[64:, 64:] = 1
    return {"memT": np.ascontiguousarray(inp["mem"][bi].T), "ones": np.ones((128, 128), NPBF), "blk64f": blk,
            "xa_wq": inp["xa_wq"][i], "xa_wkv": inp["xa_wkv"][i], "xa_wo": inp["xa_wo"][i],
            "g_x": pc(inp["xa_norm_x"][i]), "g_m": pc(inp["xa_norm_mem"][i]), "g_hq": pc(inp["xa_q_hnorm"][i]),
            "g_hk": pc(inp["xa_k_hnorm"][i]), "g_f": pc(inp["ffn_norm"][i]),
            "ffn_w13": inp["ffn_w13"][i], "ffn_w2": inp["ffn_w2"][i]}


def _run(b, maps):
    return run_bass_kernel_spmd(b.nc, maps, core_ids=list(range(8))).results


def kernel(**inputs):
    inp = {k: np.asarray(v) for k, v in inputs.items()}
    S, NT, NB = 16384, 4096, 2
    cores = list(range(8))
    ca = np.ascontiguousarray
    r1 = _run(build_front0(NT), [front0_inputs(inp, c) for c in cores])
    qT = [np.concatenate([r1[4 * bi + sg]["qT"] for sg in range(4)], axis=2) for bi in range(NB)]
    kT = [np.concatenate([r1[4 * bi + sg]["kT"] for sg in range(4)], axis=2) for bi in range(NB)]
    vT = [np.concatenate([r1[4 * bi + sg]["vT"] for sg in range(4)], axis=1) for bi in range(NB)]
    rw = [np.concatenate([r1[4 * bi + sg]["rw"] for sg in range(4)], axis=2) for bi in range(NB)]
    del r1
    masks = attn_masks()

    def make_vp(vT_h):
        v = vT_h.T.reshape(S // 128, 128, 64).transpose(1, 0, 2)
        return np.concatenate([v, np.ones_like(v)], -1)
    maps = []
    for c in cores:
        bi, hp = c // 4, c % 4
        v8 = vT[bi].reshape(8, 64, S)
        maps.append({"qT": ca(qT[bi][2 * hp:2 * hp + 2]), "kT": ca(kT[bi][2 * hp:2 * hp + 2]),
                     "vp": ca(np.stack([make_vp(v8[2 * hp + h]) for h in range(2)])), "masks": masks})
    r2 = _run(build_attn(S), maps)
    ymT = [ca(np.concatenate([r2[4 * bi + hp]["yT"].reshape(128, S) for hp in range(4)], axis=0)) for bi in range(NB)]
    del r2, maps
    r3 = _run(build_rwkv(S), [rwkv_inputs(rw[c // 4], c % 4, S) for c in cores])
    ysT = [ca(np.concatenate([rwkv_unstack(r3[4 * bi + hp]["y"], S) for hp in range(4)], axis=0)) for bi in range(NB)]
    del r3
    maps = []
    for c in cores:
        bi, sg = c // 4, c % 4
        ts_ = slice(sg * NT, (sg + 1) * NT)
        m = back_common(inp, 0, bi)
        m.update({"hT": ca(inp["x"][bi, ts_].T), "ymT": ca(ymT[bi][:, ts_]), "ysT": ca(ysT[bi][:, ts_]),
                  "gT": ca(rw[bi][6][:, ts_]), "bonT": ca(rw[bi][7][:, ts_]),
                  "ln_g": pc(inp["rw_ln_g"][0]), "ln_b": pc(inp["rw_ln_b"][0]), "w_out": inp["ev_w_out"][0]})
        maps.append(m)
    r4 = _run(build_back("even", NT), maps)
    h1T = [np.concatenate([r4[4 * bi + sg]["hT_out"] for sg in range(4)], axis=1) for bi in range(NB)]
    del r4, maps, rw, ymT, ysT, qT, kT, vT
    cw = inp["ssm_conv_w"][0]
    maps = []
    for c in cores:
        bi, sg = c // 4, c % 4
        hs = np.zeros((1024, NT + 3), np.float32)
        s0 = sg * NT
        if s0 > 0:
            hs[:, 0:3] = h1T[bi][:, s0 - 3:s0]
        hs[:, 3:] = h1T[bi][:, s0:s0 + NT]
        maps.append({"hT": hs, "od_w_in": inp["od_w_in"][0], "g_od": pc(inp["od_norm"][0]),
                     "conv_w": ca(cw.reshape(4, 24, 128).transpose(2, 1, 0)), "conv_b": pc(inp["ssm_conv_b"][0]),
                     "dt_bias": inp["ssm_dt_bias"][0].reshape(32, 1).astype(np.float32), "ones": np.ones((128, 128), NPBF)})
    r5 = _run(build_front1(NT), maps)
    zsT = [np.concatenate([r5[4 * bi + sg]["zsT"] for sg in range(4)], axis=1) for bi in range(NB)]
    xbcT = [np.concatenate([r5[4 * bi + sg]["xbcT"] for sg in range(4)], axis=1) for bi in range(NB)]
    dtT = [np.concatenate([r5[4 * bi + sg]["dtT"] for sg in range(4)], axis=1) for bi in range(NB)]
    del r5, maps
    r6 = _run(build_ssd(S), [ssd_inputs(xbcT[c // 4], dtT[c // 4], inp["ssm_a_log"][0], inp["ssm_d"][0], c % 4, S) for c in cores])
    yT = [ca(np.concatenate([ssd_unstack(r6[4 * bi + g]["y"], S) for g in range(4)], axis=1).T) for bi in range(NB)]
    del r6, xbcT, dtT
    maps = []
    for c in cores:
        bi, sg = c // 4, c % 4
        ts_ = slice(sg * NT, (sg + 1) * NT)
        m = back_common(inp, 1, bi)
        m.update({"hT": ca(h1T[bi][:, ts_]), "yT": ca(yT[bi][:, ts_]), "zsT": ca(zsT[bi][:, ts_]),
                  "gnorm": pc(inp["ssm_gnorm"][0]), "w_out": inp["od_w_out"][0]})
        maps.append(m)
    r7 = _run(build_back("odd", NT), maps)
    out = np.zeros((NB, S, 1024), np.float32)
    for c in cores:
        bi, sg = c // 4, c % 4
        out[bi, sg * NT:(sg + 1) * NT] = r7[c]["hT_out"].T
    return out
```
